# Optimizing a Trainium2 kernel written in Bass

```python
import jax, jax.numpy as jnp
from jax import lax
import numpy as np

D_MODEL = 1024
BATCH = 4
SEQ = 4096
DEPTH = 2

GDN_HEADS = 4
GDN_DK = 128
GDN_DV = 128
GDN_CHUNK = 64
CONV_K = 4
MLA_HEADS = 8
MLA_Q_LORA = 384
MLA_KV_LORA = 256
MLA_NOPE = 128
MLA_ROPE = 64
MLA_DV = 128
ROPE_THETA = 10000.0
Q_BLOCK = 128
LRU_WIDTH = 512
LRU_BLOCKS = 8
LRU_BW = LRU_WIDTH // LRU_BLOCKS
RG_C = 8.0

RMS_EPS = 1e-6
L2_EPS = 1e-6

GDN_WIDTH = GDN_HEADS * GDN_DV
MLA_WIDTH = MLA_HEADS * MLA_DV
MIX_WIDTH = GDN_WIDTH + MLA_WIDTH + LRU_WIDTH
MLA_QK = MLA_NOPE + MLA_ROPE
IN_SPLITS = (GDN_HEADS * GDN_DK, GDN_HEADS * GDN_DK, GDN_WIDTH, GDN_HEADS, GDN_HEADS, GDN_WIDTH,
             MLA_Q_LORA, MLA_KV_LORA, MLA_ROPE, MLA_WIDTH,
             LRU_WIDTH, LRU_WIDTH)
IN_WIDTH = sum(IN_SPLITS)

kernel_name = "hybrid_gdn_mla_rglru_parallel_heads"


def rmsnorm(x, gain):
    xf = x.astype(jnp.float32)
    y = xf * lax.rsqrt(jnp.mean(xf * xf, axis=-1, keepdims=True) + RMS_EPS)
    return (y * gain.astype(jnp.float32)).astype(x.dtype)


def l2norm(x):
    return x * lax.rsqrt(jnp.sum(x * x, axis=-1, keepdims=True) + L2_EPS)


def causal_dwconv(x, w):
    K, C = w.shape
    return lax.conv_general_dilated(
        x, w[:, None, :].astype(x.dtype), window_strides=(1,), padding=[(K - 1, 0)],
        dimension_numbers=("NWC", "WIO", "NWC"), feature_group_count=C)


def gated_delta_rule(q, k, v, g, beta):
    B, S, H, DK = q.shape
    DV = v.shape[-1]
    C = GDN_CHUNK
    N = S // C

    def chunks(t):
        return t.reshape(B, N, C, H, t.shape[-1]).transpose(0, 1, 3, 2, 4)

    qc = chunks(q * DK ** -0.5)
    kc = chunks(k)
    vc = chunks(v)
    gc = jnp.cumsum(g.reshape(B, N, C, H).transpose(0, 1, 3, 2), axis=-1)
    bc = beta.reshape(B, N, C, H).transpose(0, 1, 3, 2)[..., None]
    tril = jnp.tril(jnp.ones((C, C), bool))
    strict = jnp.tril(jnp.ones((C, C), bool), -1)
    decay = jnp.exp(jnp.where(tril, gc[..., :, None] - gc[..., None, :], -jnp.inf))
    kbeta = kc * bc
    a_mat = jnp.where(strict, jnp.einsum('bnhid,bnhjd->bnhij', kbeta, kc) * decay, 0.0) \
        + jnp.eye(C, dtype=q.dtype)
    rhs = jnp.concatenate([vc * bc, kbeta * jnp.exp(gc)[..., None]], axis=-1)
    sol = lax.linalg.triangular_solve(a_mat, rhs, left_side=True, lower=True, unit_diagonal=True)
    u, w = sol[..., :DV], sol[..., DV:]
    attn = jnp.where(tril, jnp.einsum('bnhid,bnhjd->bnhij', qc, kc) * decay, 0.0)

    def step(state, inp):
        q_n, k_n, u_n, w_n, g_n, a_n = inp
        v_new = u_n - jnp.einsum('bhck,bhkv->bhcv', w_n, state)
        o = jnp.einsum('bhck,bhkv->bhcv', q_n * jnp.exp(g_n)[..., None], state) \
            + jnp.einsum('bhij,bhjv->bhiv', a_n, v_new)
        g_last = g_n[..., -1]
        state = state * jnp.exp(g_last)[..., None, None] + jnp.einsum(
            'bhck,bhcv->bhkv', k_n * jnp.exp(g_last[..., None] - g_n)[..., None], v_new)
        return state, o

    xs = (jnp.moveaxis(qc, 1, 0), jnp.moveaxis(kc, 1, 0), jnp.moveaxis(u, 1, 0),
          jnp.moveaxis(w, 1, 0), jnp.moveaxis(gc, 1, 0), jnp.moveaxis(attn, 1, 0))
    state0 = jnp.zeros((B, H, DK, DV), q.dtype)
    _, o = lax.scan(step, state0, xs)
    return o.transpose(1, 0, 3, 2, 4).reshape(B, S, H, DV)


def rope_tables(S):
    inv = 1.0 / (ROPE_THETA ** (jnp.arange(0, MLA_ROPE, 2, dtype=jnp.float32) / MLA_ROPE))
    ang = jnp.arange(S, dtype=jnp.float32)[:, None] * inv[None, :]
    ang = jnp.concatenate([ang, ang], axis=-1)
    return jnp.cos(ang)[None, :, None, :], jnp.sin(ang)[None, :, None, :]


def apply_rope(t, cos, sin):
    half = t.shape[-1] // 2
    rot = jnp.concatenate([-t[..., half:], t[..., :half]], axis=-1)
    return (t * cos + rot * sin).astype(t.dtype)


def causal_block_attention(q, k, v):
    B, H, S, D = q.shape
    nb = S // Q_BLOCK
    scale = D ** -0.5
    qb = q.reshape(B, H, nb, Q_BLOCK, D).transpose(2, 0, 1, 3, 4)
    kpos = jnp.arange(S)

    def one_block(args):
        qi, i = args
        s = jnp.einsum('bhqd,bhkd->bhqk', qi, k).astype(jnp.float32) * scale
        qpos = i * Q_BLOCK + jnp.arange(Q_BLOCK)
        s = jnp.where(kpos[None, :] <= qpos[:, None], s, -jnp.inf)
        p = jax.nn.softmax(s, axis=-1).astype(v.dtype)
        return jnp.einsum('bhqk,bhkd->bhqd', p, v)

    out = lax.map(one_block, (qb, jnp.arange(nb)))
    return out.transpose(1, 0, 3, 2, 4).reshape(B, S, H * v.shape[-1])


def rg_lru(x, w_r, b_r, w_i, b_i, a_param):
    B, S, W = x.shape
    xb = x.reshape(B, S, LRU_BLOCKS, LRU_BW)
    r = jax.nn.sigmoid(jnp.einsum('bsnc,ncd->bsnd', xb, w_r).reshape(B, S, W) + b_r)
    i = jax.nn.sigmoid(jnp.einsum('bsnc,ncd->bsnd', xb, w_i).reshape(B, S, W) + b_i)
    log_a = -RG_C * r * jax.nn.softplus(-a_param)
    a = jnp.exp(log_a)
    b = jnp.sqrt(-jnp.expm1(2.0 * log_a)) * (i * x)

    def combine(lhs, rhs):
        a1, b1 = lhs
        a2, b2 = rhs
        return a1 * a2, a2 * b1 + b2

    _, h = lax.associative_scan(combine, (a, b), axis=1)
    return h


def hybrid_layer(x, cos, sin, norm_gain, w_in, gdn_conv_w, gdn_a_log, gdn_dt_bias, gdn_out_norm,
                 mla_q_norm, mla_w_uq, mla_kv_norm, mla_w_ukv, lru_conv_w, lru_conv_b,
                 lru_w_r, lru_b_r, lru_w_i, lru_b_i, lru_a_param, w_out):
    B, S, _ = x.shape
    dt = x.dtype
    h = rmsnorm(x, norm_gain)
    proj = h @ w_in
    offsets = [int(o) for o in np.cumsum(IN_SPLITS)[:-1]]
    (gq, gk, gv, gb, ga, gz, mq, mkv, mkr, mz, lx, lz) = jnp.split(proj, offsets, axis=-1)

    qkv = jax.nn.silu(causal_dwconv(jnp.concatenate([gq, gk, gv], axis=-1), gdn_conv_w))
    qkv = qkv.astype(jnp.float32)
    q = l2norm(qkv[..., :GDN_HEADS * GDN_DK].reshape(B, S, GDN_HEADS, GDN_DK))
    k = l2norm(qkv[..., GDN_HEADS * GDN_DK:2 * GDN_HEADS * GDN_DK].reshape(B, S, GDN_HEADS, GDN_DK))
    v = qkv[..., 2 * GDN_HEADS * GDN_DK:].reshape(B, S, GDN_HEADS, GDN_DV)
    beta = jax.nn.sigmoid(gb.astype(jnp.float32))
    g = -jnp.exp(gdn_a_log.astype(jnp.float32)) * jax.nn.softplus(
        ga.astype(jnp.float32) + gdn_dt_bias.astype(jnp.float32))
    o_a = gated_delta_rule(q, k, v, g, beta)
    o_a = rmsnorm(o_a, gdn_out_norm).reshape(B, S, GDN_WIDTH).astype(dt) * jax.nn.silu(gz)

    qm = (rmsnorm(mq, mla_q_norm) @ mla_w_uq).reshape(B, S, MLA_HEADS, MLA_QK)
    kv = (rmsnorm(mkv, mla_kv_norm) @ mla_w_ukv).reshape(B, S, MLA_HEADS, MLA_NOPE + MLA_DV)
    q_pe = apply_rope(qm[..., MLA_NOPE:], cos, sin)
    k_pe = apply_rope(mkr[:, :, None, :], cos, sin)
    qf = jnp.concatenate([qm[..., :MLA_NOPE], q_pe], axis=-1)
    kf = jnp.concatenate([kv[..., :MLA_NOPE],
                          jnp.broadcast_to(k_pe, (B, S, MLA_HEADS, MLA_ROPE))], axis=-1)
    vm = kv[..., MLA_NOPE:]
    o_b = causal_block_attention(qf.transpose(0, 2, 1, 3), kf.transpose(0, 2, 1, 3),
                                 vm.transpose(0, 2, 1, 3)).astype(dt) * jax.nn.silu(mz)

    u = (causal_dwconv(lx, lru_conv_w) + lru_conv_b).astype(jnp.float32)
    o_c = rg_lru(u, lru_w_r.astype(jnp.float32), lru_b_r.astype(jnp.float32),
                 lru_w_i.astype(jnp.float32), lru_b_i.astype(jnp.float32),
                 lru_a_param.astype(jnp.float32)).astype(dt) * jax.nn.silu(lz)

    mixed = jnp.concatenate([o_a, o_b, o_c], axis=-1)
    return x + mixed @ w_out


def setup_inputs(seed: int = 0) -> dict:
    key = jax.random.key(seed)
    ks = jax.random.split(key, 24)
    L = DEPTH
    f32 = jnp.float32

    def nrm(k, shape, scale):
        return scale * jax.random.normal(k, shape, f32)

    def gain(k, shape):
        return 1.0 + 0.01 * jax.random.normal(k, shape, f32)

    x = jax.random.normal(ks[0], (BATCH, SEQ, D_MODEL), f32)
    norm_gain = gain(ks[1], (L, D_MODEL))
    w_in = nrm(ks[2], (L, D_MODEL, IN_WIDTH), D_MODEL ** -0.5)
    gdn_conv_w = nrm(ks[3], (L, CONV_K, 2 * GDN_HEADS * GDN_DK + GDN_WIDTH), CONV_K ** -0.5)
    gdn_a_log = jnp.log(jax.random.uniform(ks[4], (L, GDN_HEADS), f32, 1.0, 16.0))
    dt0 = jnp.exp(jax.random.uniform(ks[5], (L, GDN_HEADS), f32, np.log(1e-3), np.log(1e-1)))
    gdn_dt_bias = dt0 + jnp.log(-jnp.expm1(-dt0))
    gdn_out_norm = gain(ks[6], (L, GDN_DV))
    mla_q_norm = gain(ks[7], (L, MLA_Q_LORA))
    mla_w_uq = nrm(ks[8], (L, MLA_Q_LORA, MLA_HEADS * MLA_QK), MLA_Q_LORA ** -0.5)
    mla_kv_norm = gain(ks[9], (L, MLA_KV_LORA))
    mla_w_ukv = nrm(ks[10], (L, MLA_KV_LORA, MLA_HEADS * (MLA_NOPE + MLA_DV)), MLA_KV_LORA ** -0.5)
    lru_conv_w = nrm(ks[11], (L, CONV_K, LRU_WIDTH), CONV_K ** -0.5)
    lru_conv_b = nrm(ks[12], (L, LRU_WIDTH), 0.01)
    lru_w_r = nrm(ks[13], (L, LRU_BLOCKS, LRU_BW, LRU_BW), LRU_BW ** -0.5)
    lru_b_r = nrm(ks[14], (L, LRU_WIDTH), 0.01)
    lru_w_i = nrm(ks[15], (L, LRU_BLOCKS, LRU_BW, LRU_BW), LRU_BW ** -0.5)
    lru_b_i = nrm(ks[16], (L, LRU_WIDTH), 0.01)
    a_target = jax.random.uniform(ks[17], (L, LRU_WIDTH), f32, 0.9, 0.999)
    sp = -jnp.log(a_target) / RG_C
    lru_a_param = -jnp.log(jnp.expm1(sp))
    w_out = nrm(ks[18], (L, MIX_WIDTH, D_MODEL), MIX_WIDTH ** -0.5)
    final_norm = gain(ks[19], (D_MODEL,))
    return {"x": x, "norm_gain": norm_gain, "w_in": w_in, "gdn_conv_w": gdn_conv_w,
            "gdn_a_log": gdn_a_log, "gdn_dt_bias": gdn_dt_bias, "gdn_out_norm": gdn_out_norm,
            "mla_q_norm": mla_q_norm, "mla_w_uq": mla_w_uq, "mla_kv_norm": mla_kv_norm,
            "mla_w_ukv": mla_w_ukv, "lru_conv_w": lru_conv_w, "lru_conv_b": lru_conv_b,
            "lru_w_r": lru_w_r, "lru_b_r": lru_b_r, "lru_w_i": lru_w_i, "lru_b_i": lru_b_i,
            "lru_a_param": lru_a_param, "w_out": w_out, "final_norm": final_norm}


def reference(x, norm_gain, w_in, gdn_conv_w, gdn_a_log, gdn_dt_bias, gdn_out_norm,
              mla_q_norm, mla_w_uq, mla_kv_norm, mla_w_ukv, lru_conv_w, lru_conv_b,
              lru_w_r, lru_b_r, lru_w_i, lru_b_i, lru_a_param, w_out, final_norm):
    cos, sin = rope_tables(x.shape[1])
    h = x
    for l in range(DEPTH):
        h = hybrid_layer(h, cos, sin, norm_gain[l], w_in[l], gdn_conv_w[l], gdn_a_log[l],
                         gdn_dt_bias[l], gdn_out_norm[l], mla_q_norm[l], mla_w_uq[l],
                         mla_kv_norm[l], mla_w_ukv[l], lru_conv_w[l], lru_conv_b[l],
                         lru_w_r[l], lru_b_r[l], lru_w_i[l], lru_b_i[l], lru_a_param[l], w_out[l])
    return rmsnorm(h, final_norm)
```

```python
from concourse.bass_utils import run_bass_kernel_spmd
import numpy as np
import concourse.bass as bass
import concourse.mybir as mybir
from contextlib import ExitStack

F32 = mybir.dt.float32
BF16 = mybir.dt.bfloat16
AF = mybir.ActivationFunctionType
ALU = mybir.AluOpType
AX = mybir.AxisListType

SAME_ENGINE_SYNC = True


def _region(ap):
    t = ap.tensor
    shape = list(t.shape)
    apl = list(ap.ap)
    off = int(ap.offset)
    space = str(ap.space)
    if space in ("SB", "PSUM"):
        C = int(apl[0][0]) if len(apl) > 0 and apl[0][0] > 0 else int(np.prod(shape[1:]))
        C = int(np.prod(shape[1:]))
    else:
        C = int(shape[-1])
    r0 = off // C
    c0 = off % C
    rext = 0
    cext = 0
    for (st, cnt) in apl:
        st = int(st); cnt = int(cnt)
        if cnt <= 1:
            continue
        if st % C == 0 and st != 0:
            rext += (abs(st) // C) * (cnt - 1)
        else:
            cext += abs(st) * (cnt - 1)
    if c0 + cext >= C:
        rext += (c0 + cext) // C
        c0 = 0
        cext = C - 1
    if space == "PSUM":
        return (ap.name, 0, 128, 0, C)
    return (ap.name, r0, r0 + rext + 1, c0, c0 + cext + 1)


def _overlap(a, b):
    return a[1] < b[2] and b[1] < a[2] and a[3] < b[4] and b[3] < a[4]


def _contains(a, b):
    return a[1] <= b[1] and b[2] <= a[2] and a[3] <= b[3] and b[4] <= a[4]


class Prog:
    ENG = ("pe", "dve", "act", "pool", "sp")

    def __init__(self, nc, es, ndma_sems=8):
        self.nc = nc
        self.es = es
        self.eng = {"pe": nc.tensor, "dve": nc.vector, "act": nc.scalar, "pool": nc.gpsimd, "sp": nc.sync}
        self.sem = {e: es.enter_context(nc.semaphore("s_" + e)) for e in self.ENG}
        self.seq = {e: 0 for e in self.ENG}
        self.dma_sems = {}
        self.dma_cnt = {}
        for e in ("sp", "pool", "act"):
            self.dma_sems[e] = [es.enter_context(nc.semaphore("d_%s%d" % (e, i))) for i in range(ndma_sems)]
            self.dma_cnt[e] = 0
        self.K = ndma_sems
        self.seen = {e: {} for e in self.ENG}
        self.semobj = {}
        for e in self.ENG:
            self.semobj["s_" + e] = self.sem[e]
        for e in self.dma_sems:
            for i, s in enumerate(self.dma_sems[e]):
                self.semobj["d_%s%d" % (e, i)] = s
        self.acc = {}
        self.ninstr = 0
        self.nwait = 0

    def _need(self, eng, reads, writes):
        need = {}
        for ap in reads:
            rg = _region(ap)
            for (r, w, sn, v, e) in self.acc.get(rg[0], ()):
                if w and _overlap(r, rg):
                    if need.get(sn, 0) < v:
                        need[sn] = v
        for ap in writes:
            rg = _region(ap)
            for (r, w, sn, v, e) in self.acc.get(rg[0], ()):
                if _overlap(r, rg):
                    if need.get(sn, 0) < v:
                        need[sn] = v
        return need

    def _emit_waits(self, eng, need, is_pe_acc=False):
        own = "s_" + eng
        for sn, v in need.items():
            if sn == own:
                if not SAME_ENGINE_SYNC or eng == "pe":
                    continue
            if self.seen[eng].get(sn, 0) >= v:
                continue
            self.eng[eng].wait_ge(self.semobj[sn], v)
            self.seen[eng][sn] = v
            self.nwait += 1

    def _record(self, reads, writes, sn, v, eng):
        for ap in reads:
            rg = _region(ap)
            lst = self.acc.setdefault(rg[0], [])
            lst[:] = [x for x in lst if not ((not x[1]) and x[2] == sn and _contains(rg, x[0]))]
            lst.append((rg, False, sn, v, eng))
        for ap in writes:
            rg = _region(ap)
            lst = self.acc.setdefault(rg[0], [])
            lst[:] = [x for x in lst if not _contains(rg, x[0])]
            lst.append((rg, True, sn, v, eng))

    def op(self, eng, fn, reads, writes):
        ps_r = [a for a in reads if str(a.space) == "PSUM"]
        if ps_r:
            reads = [a for a in reads if str(a.space) != "PSUM"]
            writes = list(writes) + ps_r
        need = self._need(eng, reads, writes)
        self._emit_waits(eng, need)
        self.seq[eng] += 1
        v = self.seq[eng]
        ins = fn()
        ins.then_inc(self.sem[eng], 1)
        self._record(reads, writes, "s_" + eng, v, eng)
        self.ninstr += 1
        return ins

    def dma(self, q, out, in_, **kw):
        need = self._need(q, [in_], [out])
        i = self.dma_cnt[q]
        k = i % self.K
        sn = "d_%s%d" % (q, k)
        prev = 16 * (i // self.K)
        if prev > 0 and need.get(sn, 0) < prev:
            need[sn] = prev
        for s, v in need.items():
            if self.seen[q].get(s, 0) >= v:
                continue
            self.eng[q].wait_ge(self.semobj[s], v)
            self.seen[q][s] = v
            self.nwait += 1
        val = prev + 16
        self.eng[q].dma_start(out=out, in_=in_, **kw).then_inc(self.semobj[sn], 16)
        self.dma_cnt[q] = i + 1
        self._record([in_], [out], sn, val, q)
        self.ninstr += 1

    def barrier(self):
        for e in self.ENG:
            for f in self.ENG:
                if f == e or self.seq[f] == 0:
                    continue
                sn = "s_" + f
                if self.seen[e].get(sn, 0) < self.seq[f]:
                    self.eng[e].wait_ge(self.sem[f], self.seq[f])
                    self.seen[e][sn] = self.seq[f]
            for q in self.dma_sems:
                n = self.dma_cnt[q]
                for k in range(self.K):
                    uses = (n - k + self.K - 1) // self.K if n > k else 0
                    if uses > 0:
                        v = 16 * uses
                        sn = "d_%s%d" % (q, k)
                        if self.seen[e].get(sn, 0) < v:
                            self.eng[e].wait_ge(self.dma_sems[q][k], v)
                            self.seen[e][sn] = v
        self.acc = {}

    def finish(self):
        for q in self.dma_sems:
            n = self.dma_cnt[q]
            for k in range(self.K):
                uses = (n - k + self.K - 1) // self.K if n > k else 0
                if uses > 0:
                    v = 16 * uses
                    if self.seen["sp"].get("d_%s%d" % (q, k), 0) < v:
                        self.eng["sp"].wait_ge(self.dma_sems[q][k], v)
        for e in self.ENG:
            if self.seq[e] > 0 and e != "sp":
                self.eng["sp"].wait_ge(self.sem[e], self.seq[e])

    def mm(self, out, lhsT, rhs, start=True, stop=True):
        rd = [lhsT, rhs] + ([] if start else [out])
        return self.op("pe", lambda: self.nc.tensor.matmul(out, lhsT, rhs, start=start, stop=stop), rd, [out])

    def tr(self, out, in_, ident):
        return self.op("pe", lambda: self.nc.tensor.transpose(out, in_, ident), [in_, ident], [out])

    def act(self, out, in_, func, bias=0.0, scale=1.0, accum_out=None, eng="act"):
        rd = [in_]
        if not isinstance(bias, (int, float)):
            rd.append(bias)
        if not isinstance(scale, (int, float)):
            rd.append(scale)
        wr = [out] + ([accum_out] if accum_out is not None else [])
        kw = {}
        if accum_out is not None:
            kw["accum_out"] = accum_out
        return self.op("act", lambda: self.nc.scalar.activation(out, in_, func, bias=bias, scale=scale, **kw), rd, wr)

    def tt(self, eng, out, in0, in1, op):
        return self.op(eng, lambda: self.eng[eng].tensor_tensor(out, in0, in1, op), [in0, in1], [out])

    def ts(self, eng, out, in0, s1, s2, op0, op1=None, accum_out=None):
        rd = [in0]
        if not isinstance(s1, (int, float)):
            rd.append(s1)
        if s2 is not None and not isinstance(s2, (int, float)):
            rd.append(s2)
        wr = [out] + ([accum_out] if accum_out is not None else [])
        kw = {}
        if op1 is not None:
            kw["op1"] = op1
        if accum_out is not None:
            kw["accum_out"] = accum_out
        return self.op(eng, lambda: self.eng[eng].tensor_scalar(out, in0, s1, s2, op0, **kw), rd, wr)

    def stt(self, out, in0, scalar, in1, op0, op1, eng="dve"):
        rd = [in0, in1]
        if not isinstance(scalar, (int, float)):
            rd.append(scalar)
        return self.op(eng, lambda: self.eng[eng].scalar_tensor_tensor(out, in0, scalar, in1, op0, op1), rd, [out])

    def copy(self, eng, out, in_):
        if eng == "act":
            return self.op("act", lambda: self.nc.scalar.copy(out, in_), [in_], [out])
        return self.op(eng, lambda: self.eng[eng].tensor_copy(out, in_), [in_], [out])

    def memset(self, eng, out, val):
        return self.op(eng, lambda: self.eng[eng].memset(out, val), [], [out])

    def recip(self, out, in_):
        return self.op("dve", lambda: self.nc.vector.reciprocal(out, in_), [in_], [out])

    def scan(self, out, d0, d1, init, op0, op1):
        rd = [d0, d1] + ([] if isinstance(init, (int, float)) else [init])
        return self.op("dve", lambda: self.nc.vector.tensor_tensor_scan(out, d0, d1, init, op0, op1), rd, [out])


D = 1024
GDN_H = 4
MLA_H = 8
EPS = 1e-6
DT_G = F32


_UIDC = [0]


def _uid(inc=False):
    return _UIDC[0]


class Cfg:
    def __init__(self, gdn_heads, mla_heads, lru_chunks):
        self.gh = list(gdn_heads)
        self.mh = list(mla_heads)
        self.lc = list(lru_chunks)
        self.HG = len(self.gh); self.HM = len(self.mh); self.NL = len(self.lc)
        HG, HM, NL = self.HG, self.HM, self.NL
        self.c_q = 0; self.c_k = HG; self.c_v = 2 * HG; self.c_gz = 3 * HG
        self.c_gba = 4 * HG
        self.c_mq = 4 * HG + 1; self.c_mkv = 4 * HG + 4; self.c_mkr = 4 * HG + 6
        self.c_mz = 4 * HG + 7
        self.c_lx = self.c_mz + HM; self.c_lz = self.c_lx + NL
        self.NCH = self.c_lz + NL
        self.MR = HG + HM + NL


O_GQ, O_GK, O_GV, O_GB, O_GA, O_GZ = 0, 512, 1024, 1536, 1540, 1544
O_MQ, O_MKV, O_MKR, O_MZ, O_LX, O_LZ = 2056, 2440, 2696, 2760, 3784, 4296


def in_col_index(cfg):
    idx = -np.ones(cfg.NCH * 128, dtype=np.int64)
    r = np.arange(128)
    for i, h in enumerate(cfg.gh):
        idx[(cfg.c_q + i) * 128 + r] = O_GQ + h * 128 + r
        idx[(cfg.c_k + i) * 128 + r] = O_GK + h * 128 + r
        idx[(cfg.c_v + i) * 128 + r] = O_GV + h * 128 + r
        idx[(cfg.c_gz + i) * 128 + r] = O_GZ + h * 128 + r
        idx[cfg.c_gba * 128 + i] = O_GB + h
        idx[cfg.c_gba * 128 + 4 + i] = O_GA + h
    idx[cfg.c_mq * 128 + np.arange(384)] = O_MQ + np.arange(384)
    idx[cfg.c_mkv * 128 + np.arange(256)] = O_MKV + np.arange(256)
    j = np.arange(64)
    idx[cfg.c_mkr * 128 + j] = O_MKR + j
    idx[cfg.c_mkr * 128 + 64 + j] = O_MKR + (j + 32) % 64
    for i, h in enumerate(cfg.mh):
        idx[(cfg.c_mz + i) * 128 + r] = O_MZ + h * 128 + r
    for i, c in enumerate(cfg.lc):
        idx[(cfg.c_lx + i) * 128 + r] = O_LX + c * 128 + r
        idx[(cfg.c_lz + i) * 128 + r] = O_LZ + c * 128 + r
    return idx


def gather_cols(w, idx):
    out = np.zeros(w.shape[:-1] + (len(idx),), dtype=w.dtype)
    m = idx >= 0
    out[..., m] = w[..., idx[m]]
    return out


def host_prep(inp, cfg, T):
    L = inp["w_in"].shape[0]
    HG, HM, NL = cfg.HG, cfg.HM, cfg.NL
    f = np.float32
    o = {}
    o["w_in"] = np.ascontiguousarray(gather_cols(inp["w_in"], in_col_index(cfg)))
    o["gain_in"] = np.ascontiguousarray(inp["norm_gain"].reshape(L, 8, 128).transpose(0, 2, 1))
    gcw = np.zeros((L, 128, 3 * HG, 4), f)
    for ty in range(3):
        for i, h in enumerate(cfg.gh):
            ch = ty * 512 + h * 128 + np.arange(128)
            gcw[:, :, ty * HG + i, :] = inp["gdn_conv_w"][:, :, ch].transpose(0, 2, 1)
    o["gcw"] = gcw
    alog8 = np.zeros((L, 8, 1), f); dtb8 = np.zeros((L, 8, 1), f)
    for i, h in enumerate(cfg.gh):
        alog8[:, 4 + i, 0] = inp["gdn_a_log"][:, h]
        dtb8[:, 4 + i, 0] = inp["gdn_dt_bias"][:, h]
    o["alog8"] = alog8; o["dtb8"] = dtb8
    o["gnorm"] = np.ascontiguousarray(inp["gdn_out_norm"][:, :, None])
    o["qg"] = np.ascontiguousarray(inp["mla_q_norm"].reshape(L, 3, 128).transpose(0, 2, 1))
    o["kvg"] = np.ascontiguousarray(inp["mla_kv_norm"].reshape(L, 2, 128).transpose(0, 2, 1))
    uq = inp["mla_w_uq"]
    cols = []
    for h in cfg.mh:
        cols += list(h * 192 + np.arange(128))
    for h in cfg.mh:
        cols += list(h * 192 + 128 + np.arange(64))
    for h in cfg.mh:
        cols += list(h * 192 + 128 + (np.arange(64) + 32) % 64)
    o["w_uq"] = np.ascontiguousarray(uq[:, :, np.array(cols)])
    ukv = inp["mla_w_ukv"]
    cols = []
    for h in cfg.mh:
        cols += list(h * 256 + np.arange(128))
    for h in cfg.mh:
        cols += list(h * 256 + 128 + np.arange(128))
    o["w_ukv"] = np.ascontiguousarray(ukv[:, :, np.array(cols)])
    lcw = np.zeros((L, 128, NL, 4), f); lcb = np.zeros((L, 128, NL), f)
    wr = np.zeros((L, 128, NL, 128), f); wi = np.zeros((L, 128, NL, 128), f)
    lbr = np.zeros((L, 128, NL), f); lbi = np.zeros((L, 128, NL), f); lap = np.zeros((L, 128, NL), f)
    for i, c in enumerate(cfg.lc):
        ch = c * 128 + np.arange(128)
        lcw[:, :, i, :] = inp["lru_conv_w"][:, :, ch].transpose(0, 2, 1)
        lcb[:, :, i] = inp["lru_conv_b"][:, ch]
        lbr[:, :, i] = inp["lru_b_r"][:, ch]
        lbi[:, :, i] = inp["lru_b_i"][:, ch]
        lap[:, :, i] = inp["lru_a_param"][:, ch]
        for b in range(2):
            wr[:, b * 64:(b + 1) * 64, i, b * 64:(b + 1) * 64] = inp["lru_w_r"][:, 2 * c + b]
            wi[:, b * 64:(b + 1) * 64, i, b * 64:(b + 1) * 64] = inp["lru_w_i"][:, 2 * c + b]
    o["lcw"] = lcw; o["lcb"] = lcb; o["lwr"] = wr; o["lwi"] = wi; o["lbr"] = lbr; o["lbi"] = lbi; o["lap"] = lap
    rows = []
    for h in cfg.gh:
        rows += list(h * 128 + np.arange(128))
    for h in cfg.mh:
        rows += list(512 + h * 128 + np.arange(128))
    for c in cfg.lc:
        rows += list(1536 + c * 128 + np.arange(128))
    o["w_out"] = np.ascontiguousarray(inp["w_out"][:, np.array(rows), :])
    o["fgain"] = np.ascontiguousarray(np.broadcast_to(inp["final_norm"][None, :], (128, 1024))).astype(f)
    o["ident"] = np.eye(128, dtype=f)
    o["ones"] = np.ones((128, 128), f)
    ii = np.arange(128)[:, None]; jj = np.arange(128)[None, :]
    mlu = np.zeros((128, 256), f)
    mlu[:, :128] = -1.0 * (ii > jj)
    mlu[:, 128:] = 1.0 * (jj >= ii)
    o["masklu"] = mlu
    o["triu"] = (1.0 * (jj >= ii)).astype(f)
    selB = np.zeros((8, 4), f); selG = np.zeros((8, 4), f)
    for i in range(4):
        selB[i, i] = 1.0; selG[4 + i, i] = 1.0
    o["selBG"] = np.concatenate([selB, selG], axis=1)
    sel2 = np.zeros((16, 4, 4), f)
    for i in range(4):
        sel2[4 + i, i, 0] = 1.0; sel2[8, i, 1] = 1.0
        sel2[8, i, 2] = 1.0; sel2[4 + i, i, 3] = -1.0
    o["sel2"] = sel2.reshape(16, 16)
    inv = (1.0 / (10000.0 ** (np.arange(0, 64, 2, dtype=f) / f(64)))).astype(f)
    ang = (np.arange(T, dtype=f)[:, None] * inv[None, :]).astype(f)
    ang = np.concatenate([ang, ang], axis=-1)
    cosT = np.cos(ang).astype(f).T
    sinT = np.sin(ang).astype(f).T.copy()
    sinT[:32] *= -1.0
    o["cosT"] = np.ascontiguousarray(cosT); o["sinT"] = np.ascontiguousarray(sinT)
    return o


def build_nc(T, cfg, L=2, debug=False, phases="AGMLO", final=True):
    nc = bass.Bass("TRN2", target_bir_lowering=False)
    HG, HM, NL, NCH, MR = cfg.HG, cfg.HM, cfg.NL, cfg.NCH, cfg.MR
    NCOL = NCH * 128
    NT = T // 512
    dbg_kind = "ExternalOutput" if debug else "Internal"

    def din(name, shape, dt=F32):
        return nc.dram_tensor(name, list(shape), dt, kind="ExternalInput").ap()

    x_in = din("x", [T, D])
    w_in = din("w_in", [L, D, NCOL]); gain_in = din("gain_in", [L, 128, 8])
    gcw_d = din("gcw", [L, 128, 3 * HG, 4]); alog_d = din("alog8", [L, 8, 1]); dtb_d = din("dtb8", [L, 8, 1])
    gnorm_d = din("gnorm", [L, 128, 1])
    qg_d = din("qg", [L, 128, 3]); kvg_d = din("kvg", [L, 128, 2])
    wuq_d = din("w_uq", [L, 384, HM * 256]); wukv_d = din("w_ukv", [L, 256, HM * 256])
    lcw_d = din("lcw", [L, 128, NL, 4]); lcb_d = din("lcb", [L, 128, NL])
    lwr_d = din("lwr", [L, 128, NL, 128]); lwi_d = din("lwi", [L, 128, NL, 128])
    lbr_d = din("lbr", [L, 128, NL]); lbi_d = din("lbi", [L, 128, NL]); lap_d = din("lap", [L, 128, NL])
    wout_d = din("w_out", [L, MR * 128, D]); fgain_d = din("fgain", [128, D])
    ident_d = din("ident", [128, 128]); ones_d = din("ones", [128, 128]); masklu_d = din("masklu", [128, 256])
    triu_d = din("triu", [128, 128]); selBG_d = din("selBG", [8, 8]); sel2_d = din("sel2", [16, 16])
    cos_d = din("cosT", [64, T]); sin_d = din("sinT", [64, T])

    out_d = nc.dram_tensor("out", [T, D], F32, kind="ExternalOutput").ap()
    projT = nc.dram_tensor("projT", [NCOL, T], F32, kind=dbg_kind).ap()
    mixT = nc.dram_tensor("mixT", [MR * 128, T], BF16, kind=dbg_kind).ap()
    x1 = nc.dram_tensor("x1", [T, D], F32, kind=dbg_kind).ap()

    with ExitStack() as es:
        P = Prog(nc, es)
        gsb = lambda name, shape, dt: es.enter_context(nc.sbuf_tensor(name, shape, dt))
        ident_f = gsb("ident_f", [128, 128], F32); ident_b = gsb("ident_b", [128, 128], BF16)
        ones_f = gsb("ones_f", [128, 128], F32)
        masklu = gsb("masklu_s", [128, 256], F32)
        triu_b = gsb("triu_b", [128, 128], BF16); triu_f = gsb("triu_f", [128, 128], F32)
        selBG = gsb("selBG_s", [8, 8], F32); sel2 = gsb("sel2_s", [16, 16], F32)
        P.dma("sp", ident_f[:], ident_d); P.dma("sp", ones_f[:], ones_d); P.dma("sp", masklu[:], masklu_d)
        P.dma("sp", triu_f[:], triu_d); P.dma("sp", selBG[:], selBG_d); P.dma("sp", sel2[:], sel2_d)
        P.copy("dve", ident_b[:], ident_f[:]); P.copy("dve", triu_b[:], triu_f[:])
        C = dict(ident_f=ident_f, ident_b=ident_b, ones_f=ones_f, masklu=masklu, triu_b=triu_b, selBG=selBG, sel2=sel2)
        D_ = dict(projT=projT, mixT=mixT, cos=cos_d, sin=sin_d)

        for l in range(L):
            _UIDC[0] = l
            x_src = x_in if l == 0 else x1
            last = (l == L - 1)
            if "A" in phases:
                phase_A(P, nc, T, cfg, C, x_src, w_in[l], gain_in[l], projT)
                P.barrier()
            if "L" in phases:
                phase_L(P, nc, T, cfg, C, D_, lcw_d[l], lcb_d[l], lwr_d[l], lwi_d[l], lbr_d[l], lbi_d[l], lap_d[l])
                P.barrier()
            if "G" in phases:
                phase_G(P, nc, T, cfg, C, D_, gcw_d[l], alog_d[l], dtb_d[l], gnorm_d[l])
                P.barrier()
            if "M" in phases:
                phase_M(P, nc, T, cfg, C, D_, qg_d[l], kvg_d[l], wuq_d[l], wukv_d[l])
                P.barrier()
            if "O" in phases:
                dst = out_d if (last and final) else x1
                phase_O(P, nc, T, cfg, x_src, wout_d[l], mixT, dst, fgain_d if (last and final) else None)
                P.barrier()
        P.finish()
        print("ninstr", P.ninstr, "nwait", P.nwait)
    return nc


def phase_A(P, nc, T, cfg, C, x_src, w_d, gain_d, projT):
    NCH = cfg.NCH; NCOL = NCH * 128; NT = T // 512
    with ExitStack() as es:
        sb = lambda name, shape, dt: es.enter_context(nc.sbuf_tensor("A%d_" % _uid() + name, shape, dt))
        ps = lambda name, shape, dt: es.enter_context(nc.psum_tensor("A%d_" % _uid(False) + name, shape, dt))
        W = sb("W", [128, 8, NCOL], BF16)
        wst = [sb("wst%d" % i, [128, 2048], F32) for i in range(2)]
        gT = sb("gT", [128, 8], F32)
        xt = [sb("xt%d" % i, [128, D], F32) for i in range(2)]
        xs = [sb("xs%d" % i, [128, D], BF16) for i in range(2)]
        junk = sb("junk", [128, D], BF16)
        ss = [sb("ss%d" % i, [128, 1], F32) for i in range(2)]
        rstd = [sb("rstd%d" % i, [128, 1], F32) for i in range(2)]
        xnT = [sb("xnT%d" % i, [128, 8, 512], BF16) for i in range(2)]
        stage = [sb("stage%d" % i, [128, 4, 512], F32) for i in range(2)]
        pt = [ps("pt%d" % i, [128, 8, 128], BF16) for i in range(2)]
        pm = [ps("pm%d" % i, [128, 512], F32) for i in range(4)]
        P.dma("sp", gT[:], gain_d)
        k = 0
        cast_eng = ["pool", "dve", "act"]
        for kc in range(8):
            for c0 in range(0, NCOL, 2048):
                cw = min(2048, NCOL - c0)
                st = wst[k % 2]
                P.dma("sp", st[:, 0:cw], w_d[kc * 128:(kc + 1) * 128, c0:c0 + cw])
                P.copy(cast_eng[k % 3], W[:, kc, c0:c0 + cw], st[:, 0:cw])
                k += 1
        si = 0
        mi = 0
        for tt in range(NT):
            xn = xnT[tt % 2]
            for s in range(4):
                b = si % 2; si += 1
                t0 = tt * 512 + s * 128
                P.dma("sp", xt[b][:], x_src[t0:t0 + 128, :])
                P.act(junk[:], xt[b][:], AF.Square, accum_out=ss[b][:])
                P.act(rstd[b][:], ss[b][:], AF.Sqrt, bias=EPS, scale=1.0 / D)
                P.recip(rstd[b][:], rstd[b][:])
                P.ts("pool", xs[b][:], xt[b][:], rstd[b][:], None, ALU.mult)
                for kc in range(8):
                    P.tr(pt[b][:, kc, :], xs[b][:, kc * 128:(kc + 1) * 128], C["ident_b"][:])
                P.tt("dve", xn[:, :, s * 128:(s + 1) * 128], pt[b][:], gT[:].unsqueeze(2).to_broadcast([128, 8, 128]), ALU.mult)
            for m0 in range(0, NCH, 4):
                mw = min(4, NCH - m0)
                stg = stage[(m0 // 4) % 2]
                for mm_ in range(mw):
                    m = m0 + mm_
                    pp = pm[mi % 4]; mi += 1
                    for kc in range(8):
                        P.mm(pp[:], W[:, kc, m * 128:(m + 1) * 128], xn[:, kc, :], start=(kc == 0), stop=(kc == 7))
                    P.copy("act" if (m % 2) else "dve", stg[:, mm_, :], pp[:])
                P.dma("pool", projT[m0 * 128:(m0 + mw) * 128, tt * 512:(tt + 1) * 512].rearrange("(c p) t -> p c t", p=128),
                      stg[:, 0:mw, :])


def phase_O(P, nc, T, cfg, x_src, wout_d, mixT, dst, fgain_d):
    MR = cfg.MR; NT = T // 512
    with ExitStack() as es:
        sb = lambda name, shape, dt: es.enter_context(nc.sbuf_tensor("O%d_" % _uid() + name, shape, dt))
        ps = lambda name, shape, dt: es.enter_context(nc.psum_tensor("O%d_" % _uid(False) + name, shape, dt))
        Wo = sb("Wo", [128, MR, D], BF16)
        wst = [sb("wst%d" % i, [128, D], F32) for i in range(2)]
        mx = [sb("mx%d" % i, [128, MR, 512], BF16) for i in range(2)]
        xr = [sb("xr%d" % i, [128, D], F32) for i in range(2)]
        xo = [sb("xo%d" % i, [128, D], F32) for i in range(2)]
        junk = sb("junk", [128, D], BF16)
        ss = [sb("ss%d" % i, [128, 1], F32) for i in range(2)]
        po = [ps("po%d" % i, [128, 512], F32) for i in range(4)]
        if fgain_d is not None:
            fg = sb("fg", [128, D], F32)
            P.dma("sp", fg[:], fgain_d)
        cast_eng = ["pool", "dve", "act"]
        for kc in range(MR):
            st = wst[kc % 2]
            P.dma("sp", st[:], wout_d[kc * 128:(kc + 1) * 128, :])
            P.copy(cast_eng[kc % 3], Wo[:, kc, :], st[:])
        si = 0; pi = 0
        for tt in range(NT):
            m = mx[tt % 2]
            P.dma("sp", m[:], mixT[:, tt * 512:(tt + 1) * 512].rearrange("(c p) t -> p c t", p=128))
            for s in range(4):
                b = si % 2; si += 1
                t0 = tt * 512 + s * 128
                P.dma("sp", xr[b][:], x_src[t0:t0 + 128, :])
                for n in range(2):
                    pp = po[pi % 4]; pi += 1
                    for kc in range(MR):
                        P.mm(pp[:], m[:, kc, s * 128:(s + 1) * 128], Wo[:, kc, n * 512:(n + 1) * 512], start=(kc == 0), stop=(kc == MR - 1))
                    P.tt("dve", xo[b][:, n * 512:(n + 1) * 512], pp[:], xr[b][:, n * 512:(n + 1) * 512], ALU.add)
                if fgain_d is not None:
                    P.act(junk[:], xo[b][:], AF.Square, accum_out=ss[b][:])
                    P.act(ss[b][:], ss[b][:], AF.Sqrt, bias=EPS, scale=1.0 / D)
                    P.recip(ss[b][:], ss[b][:])
                    P.stt(xo[b][:], xo[b][:], ss[b][:], fg[:], ALU.mult, ALU.mult, eng="dve")
                P.dma("pool", dst[t0:t0 + 128, :], xo[b][:])


def load_halo(P, tile, projT, row0, nrows, t0):
    if t0 == 0:
        P.memset("pool", tile[0:nrows, 0:3], 0.0)
        P.dma("sp", tile[0:nrows, 3:515], projT[row0:row0 + nrows, 0:512])
    else:
        P.dma("sp", tile[0:nrows, 0:515], projT[row0:row0 + nrows, t0 - 3:t0 + 512])


def conv4(P, eng, out, src, cw, j, bias=None):
    if bias is None:
        P.ts(eng, out, src[:, 3:515], cw[:, j, 3:4], None, ALU.mult)
    else:
        P.ts(eng, out, src[:, 3:515], cw[:, j, 3:4], bias, ALU.mult, ALU.add)
    for k in range(3):
        P.stt(out, src[:, k:k + 512], cw[:, j, k:k + 1], out, ALU.mult, ALU.add, eng=eng)


def phase_L(P, nc, T, cfg, C, D_, lcw_d, lcb_d, lwr_d, lwi_d, lbr_d, lbi_d, lap_d):
    NL = cfg.NL; NT = T // 512
    projT = D_["projT"]; mixT = D_["mixT"]
    with ExitStack() as es:
        sb = lambda name, shape, dt: es.enter_context(nc.sbuf_tensor("L%d_" % _uid() + name, shape, dt))
        ps = lambda name, shape, dt: es.enter_context(nc.psum_tensor("L%d_" % _uid(False) + name, shape, dt))
        cw = sb("cw", [128, NL, 4], F32); cb = sb("cb", [128, NL], F32)
        Wr = sb("Wr", [128, NL, 128], F32); Wi = sb("Wi", [128, NL, 128], F32)
        br = sb("br", [128, NL], F32); bi = sb("bi", [128, NL], F32); ap_ = sb("ap", [128, NL], F32)
        n8 = sb("n8", [128, NL], F32); n16 = sb("n16", [128, NL], F32)
        hst = sb("hst", [128, NL], F32)
        for (t_, d_) in ((cw, lcw_d), (cb, lcb_d), (Wr, lwr_d), (Wi, lwi_d), (br, lbr_d), (bi, lbi_d), (ap_, lap_d)):
            P.dma("sp", t_[:], d_)
        P.act(n8[:], ap_[:], AF.Exp, scale=-1.0)
        P.act(n8[:], n8[:], AF.Ln, bias=1.0)
        P.ts("dve", n16[:], n8[:], -16.0, None, ALU.mult)
        P.ts("dve", n8[:], n8[:], -8.0, None, ALU.mult)
        P.memset("dve", hst[:], 0.0)
        lx = [sb("lx%d" % i, [128, 515], F32) for i in range(2)]
        lz = [sb("lz%d" % i, [128, 512], F32) for i in range(2)]
        u = [sb("u%d" % i, [128, 512], F32) for i in range(2)]
        r = sb("r", [128, 512], F32); ig = sb("ig", [128, 512], F32)
        a = sb("a", [128, 512], F32); a2 = sb("a2", [128, 512], F32)
        bb = sb("bb", [128, 512], F32); h = sb("h", [128, 512], F32)
        ob = [sb("ob%d" % i, [128, 512], BF16) for i in range(2)]
        pr = ps("pr", [128, 512], F32); pi_ = ps("pi", [128, 512], F32)
        it = 0
        for tt in range(NT):
            for n in range(NL):
                b = it % 2; it += 1
                t0 = tt * 512
                load_halo(P, lx[b], projT, (cfg.c_lx + n) * 128, 128, t0)
                P.dma("sp", lz[b][:], projT[(cfg.c_lz + n) * 128:(cfg.c_lz + n + 1) * 128, t0:t0 + 512])
                conv4(P, "dve", u[b][:], lx[b], cw, n, bias=cb[:, n:n + 1])
                P.mm(pr[:], Wr[:, n, :], u[b][:])
                P.mm(pi_[:], Wi[:, n, :], u[b][:])
                P.act(r[:], pr[:], AF.Sigmoid, bias=br[:, n:n + 1])
                P.act(ig[:], pi_[:], AF.Sigmoid, bias=bi[:, n:n + 1])
                P.act(a[:], r[:], AF.Exp, scale=n8[:, n:n + 1])
                P.act(a2[:], r[:], AF.Exp, scale=n16[:, n:n + 1])
                P.act(a2[:], a2[:], AF.Sqrt, bias=1.0, scale=-1.0)
                P.tt("pool", bb[:], ig[:], u[b][:], ALU.mult)
                P.tt("pool", bb[:], bb[:], a2[:], ALU.mult)
                P.scan(h[:], a[:], bb[:], hst[:, n:n + 1], ALU.mult, ALU.add)
                P.copy("dve", hst[:, n:n + 1], h[:, 511:512])
                P.act(lz[b][:], lz[b][:], AF.Silu)
                P.tt("dve", ob[b][:], h[:], lz[b][:], ALU.mult)
                row = (cfg.HG + cfg.HM + n) * 128
                P.dma("pool", mixT[row:row + 128, t0:t0 + 512], ob[b][:])


def phase_G(P, nc, T, cfg, C, D_, gcw_d, alog_d, dtb_d, gnorm_d):
    HG = cfg.HG; NT = T // 512
    projT = D_["projT"]; mixT = D_["mixT"]
    ident_f = C["ident_f"]; ones_f = C["ones_f"]; masklu = C["masklu"]; selBG = C["selBG"]; sel2 = C["sel2"]
    DK = 128
    with ExitStack() as es:
        sb = lambda name, shape, dt: es.enter_context(nc.sbuf_tensor("G%d_" % _uid() + name, shape, dt))
        ps = lambda name, shape, dt: es.enter_context(nc.psum_tensor("G%d_" % _uid(False) + name, shape, dt))
        gcw = sb("gcw", [128, 3 * HG, 4], F32)
        alog = sb("alog", [8, 1], F32); dtb = sb("dtb", [8, 1], F32); negA = sb("negA", [8, 1], F32)
        gn = sb("gn", [128, 1], F32)
        P.dma("sp", gcw[:], gcw_d); P.dma("sp", alog[:], alog_d); P.dma("sp", dtb[:], dtb_d); P.dma("sp", gn[:], gnorm_d)
        P.act(negA[:], alog[:], AF.Exp)
        P.ts("dve", negA[:], negA[:], -1.0, None, ALU.mult)
        ones8 = sb("ones8", [8, 128], F32)
        P.memset("dve", ones8[:], 1.0)
        X16 = sb("X16", [16, 512], F32)
        P.memset("dve", X16[:], 1.0)
        graw = sb("graw", [8, 512], F32); bsig = sb("bsig", [8, 512], F32); Gt = sb("Gt", [8, 512], F32)
        glb = sb("glb", [8, 128], F32)
        sc = sb("sc", [128, 12], F32)
        egc = sb("egc", [128, 4], F32); edgl = sb("edgl", [128, 4], F32); egl = sb("egl", [128, 4], F32)
        bg = sb("bg", [128, 4], F32)
        AR = sb("AR", [2, HG, 256], F32)
        raw = [sb("raw%d" % i, [128, 515], F32) for i in range(2)]
        cv = [sb("cv%d" % i, [128, 512], F32) for i in range(2)]
        sq = sb("sq", [128, 512], F32); rn = sb("rn", [128, 512], F32)
        QT = [sb("QT%d" % h, [128, 512], DT_G) for h in range(HG)]
        KT = [sb("KT%d" % h, [128, 512], DT_G) for h in range(HG)]
        VT = [sb("VT%d" % h, [128, 512], F32) for h in range(HG)]
        zs = [sb("zs%d" % h, [128, 512], F32) for h in range(HG)]
        ob = [sb("ob%d" % h, [128, 512], BF16) for h in range(HG)]
        Kbg = sb("Kbg", [128, 128], DT_G); Kdec = sb("Kdec", [128, 128], DT_G); Vb = sb("Vb", [128, 128], DT_G)
        E = sb("E", [128, 256], F32)
        XY = [sb("XY%d" % i, [128, 256], F32) for i in range(2)]
        RT = sb("RT", [128, 128], F32)
        attnT = sb("attnT", [128, 128], DT_G)
        UW = sb("UW", [128, 256], F32)
        vnew = sb("vnew", [128, 128], DT_G)
        o1 = sb("o1", [128, 128], F32); o = sb("o", [128, 128], F32); junk = sb("junk", [128, 128], F32)
        oss = sb("oss", [128, 1], F32)
        S = [sb("S%d" % h, [128, 128], F32) for h in range(HG)]
        for h in range(HG):
            P.memset("pool", S[h][:], 0.0)
        pbig = ps("pbig", [128, 512], F32)
        pDK = ps("pDK", [128, 512], F32)
        pX = ps("pX", [128, 512], F32)
        pU = ps("pU", [128, 512], F32)
        pS = ps("pS", [128, 512], F32)
        pO = ps("pO", [128, 512], F32)
        pA = ps("pA", [2, 512], F32)
        ri = 0
        import os
        GS = int(os.environ.get("G_STOP", "99"))
        for tt in range(NT):
            t0 = tt * 512
            if GS <= 0: return
            P.dma("sp", graw[:], projT[cfg.c_gba * 128:cfg.c_gba * 128 + 8, t0:t0 + 512])
            P.act(bsig[:], graw[:], AF.Sigmoid)
            P.act(Gt[:], graw[:], AF.Exp, bias=dtb[:])
            P.act(Gt[:], Gt[:], AF.Ln, bias=1.0)
            P.ts("dve", Gt[:], Gt[:], negA[:], None, ALU.mult)
            for c in range(4):
                P.scan(X16[0:8, c * 128:(c + 1) * 128], ones8[:], Gt[:, c * 128:(c + 1) * 128], 0.0, ALU.mult, ALU.add)
            if GS <= 1: return
            for h in range(HG):
                for ty in range(3):
                    b = ri % 2; ri += 1
                    row = (ty * HG + h) * 128
                    load_halo(P, raw[b], projT, row, 128, t0)
                    conv4(P, "dve", cv[b][:], raw[b], gcw, ty * HG + h)
                    if ty == 2:
                        P.act(VT[h][:], cv[b][:], AF.Silu)
                    else:
                        P.act(cv[b][:], cv[b][:], AF.Silu)
                        P.tt("pool", sq[:], cv[b][:], cv[b][:], ALU.mult)
                        P.mm(pbig[:], ones_f[:], sq[:])
                        P.act(rn[:], pbig[:], AF.Sqrt, bias=EPS)
                        P.recip(rn[:], rn[:])
                        dst = QT[h] if ty == 0 else KT[h]
                        P.stt(dst[:], cv[b][:], (DK ** -0.5) if ty == 0 else 1.0, rn[:], ALU.mult, ALU.mult)
                P.dma("sp", zs[h][:], projT[(cfg.c_gz + h) * 128:(cfg.c_gz + h + 1) * 128, t0:t0 + 512])
                P.act(zs[h][:], zs[h][:], AF.Silu)
            if GS <= 2: return
            for c in range(4):
                cs = slice(c * 128, (c + 1) * 128)
                P.mm(pO[:, 128:128 + 4], bsig[:, cs], selBG[:, 0:4])
                P.mm(pO[:, 132:132 + 4], X16[0:8, cs], selBG[:, 4:8])
                P.ts("dve", glb[:], ones8[:], X16[0:8, c * 128 + 127:c * 128 + 128], None, ALU.mult)
                P.mm(pO[:, 136:136 + 4], glb[:], selBG[:, 4:8])
                P.copy("dve", sc[:], pO[:, 128:140])
                P.act(egc[:], sc[:, 4:8], AF.Exp)
                P.act(egl[:], sc[:, 8:12], AF.Exp)
                P.tt("dve", edgl[:], sc[:, 8:12], sc[:, 4:8], ALU.subtract)
                P.act(edgl[:], edgl[:], AF.Exp)
                P.tt("dve", bg[:], sc[:, 0:4], egc[:], ALU.mult)
                if GS <= 3: return
                for h in range(HG):
                    P.mm(pA[:, (h % 2) * 256:(h % 2) * 256 + 128], sel2[:, h * 4:h * 4 + 2], X16[:, cs])
                    P.mm(pA[:, (h % 2) * 256 + 128:(h % 2) * 256 + 256], sel2[:, h * 4 + 2:h * 4 + 4], X16[:, cs])
                    if h % 2 == 1 or h == HG - 1:
                        h0 = h - (h % 2)
                        nh = h - h0 + 1
                        P.copy("act", AR[:, h0:h0 + nh, :], pA[:, 0:nh * 256])
                if GS <= 4: return
                for h in range(HG):
                    b_h = sc[:, h:h + 1]
                    SK = int(os.environ.get("G_SKIP", "0"))
                    if not SK & 1: P.mm(pU[:, 256:384], KT[h][:, cs], ident_f[:])
                    if not SK & 2: P.mm(pU[:, 384:512], VT[h][:, cs], ident_f[:])
                    if not SK & 4: P.ts("dve", Kbg[:], pU[:, 256:384], bg[:, h:h + 1], None, ALU.mult)
                    if not SK & 8: P.ts("dve", Kdec[:], pU[:, 256:384], edgl[:, h:h + 1], None, ALU.mult)
                    if not SK & 16: P.ts("dve", Vb[:], pU[:, 384:512], b_h, None, ALU.mult)
                    if GS <= 5: return
                    P.mm(pDK[:, 0:128], AR[:, h, 0:128], AR[:, h, 128:256])
                    P.mm(pDK[:, 128:256], AR[:, h, 128:256], AR[:, h, 0:128])
                    P.ts("dve", E[:], pDK[:, 0:256], 0.0, None, ALU.min)
                    P.act(E[:], E[:], AF.Exp)
                    P.tt("pool", E[:], E[:], masklu[:], ALU.mult)
                    P.mm(pDK[:, 256:384], KT[h][:, cs], KT[h][:, cs])
                    P.mm(pDK[:, 384:512], KT[h][:, cs], QT[h][:, cs])
                    Y0 = XY[0]
                    P.stt(Y0[:, 128:256], pDK[:, 256:384], b_h, E[:, 0:128], ALU.mult, ALU.mult)
                    P.tt("dve", attnT[:], pDK[:, 384:512], E[:, 128:256], ALU.mult)
                    P.mm(pX[:, 384:512], Y0[:, 128:256], ident_f[:])
                    P.copy("act", Y0[:, 0:128], pX[:, 384:512])
                    P.tt("dve", RT[:], pX[:, 384:512], ident_f[:], ALU.add)
                    if GS <= 6: return
                    cur = 0
                    for k in range(1, 7):
                        A_ = XY[cur]; B_ = XY[1 - cur]
                        if k < 6:
                            P.mm(pX[:, 0:128], A_[:, 128:256], A_[:, 0:128])
                        P.mm(pX[:, 128:256], A_[:, 0:128], A_[:, 128:256])
                        if k < 6:
                            P.copy("act", B_[:], pX[:, 0:256])
                        else:
                            P.copy("act", B_[:, 128:256], pX[:, 128:256])
                        P.mm(pX[:, 256:384], B_[:, 128:256], RT[:])
                        P.tt("dve", RT[:], pX[:, 256:384], RT[:], ALU.add)
                        cur = 1 - cur
                    if GS <= 7: return
                    P.mm(pU[:, 0:128], RT[:], Vb[:])
                    P.mm(pU[:, 128:256], Kbg[:], RT[:])
                    P.copy("act", UW[:], pU[:, 0:256])
                    P.mm(pS[:, 0:128], UW[:, 128:256], S[h][:])
                    P.tt("dve", vnew[:], UW[:, 0:128], pS[:, 0:128], ALU.subtract)
                    P.mm(pS[:, 128:256], QT[h][:, cs], S[h][:])
                    P.ts("dve", o1[:], pS[:, 128:256], egc[:, h:h + 1], None, ALU.mult)
                    P.mm(pS[:, 256:384], attnT[:], vnew[:])
                    P.tt("dve", o[:], pS[:, 256:384], o1[:], ALU.add)
                    P.mm(pS[:, 384:512], Kdec[:], vnew[:])
                    P.stt(S[h][:], S[h][:], egl[:, h:h + 1], pS[:, 384:512], ALU.mult, ALU.add)
                    if GS <= 8: return
                    P.act(junk[:], o[:], AF.Square, accum_out=oss[:])
                    P.act(oss[:], oss[:], AF.Sqrt, bias=EPS, scale=1.0 / 128)
                    P.recip(oss[:], oss[:])
                    P.ts("dve", o[:], o[:], oss[:], None, ALU.mult)
                    P.mm(pO[:, 0:128], o[:], ident_f[:])
                    P.stt(ob[h][:, cs], pO[:, 0:128], gn[:], zs[h][:, cs], ALU.mult, ALU.mult)
            for h in range(HG):
                P.dma("pool", mixT[h * 128:(h + 1) * 128, t0:t0 + 512], ob[h][:])


def phase_M(P, nc, T, cfg, C, D_, qg_d, kvg_d, wuq_d, wukv_d):
    HM = cfg.HM; NT = T // 512; NB = T // 128
    HMG = min(HM, 4); NG = HM // HMG
    projT = D_["projT"]; mixT = D_["mixT"]; cos_d = D_["cos"]; sin_d = D_["sin"]
    ones_f = C["ones_f"]; ident_b = C["ident_b"]; triu_b = C["triu_b"]
    SCALE = 192.0 ** -0.5
    with ExitStack() as es:
        sb = lambda name, shape, dt: es.enter_context(nc.sbuf_tensor("M%d_" % _uid() + name, shape, dt))
        ps = lambda name, shape, dt: es.enter_context(nc.psum_tensor("M%d_" % _uid(False) + name, shape, dt))
        Wuq = sb("Wuq", [128, 3, HM * 256], BF16)
        Wukv = sb("Wukv", [128, 2, HM * 256], BF16)
        wst = [sb("wst%d" % i, [128, HM * 256], F32) for i in range(2)]
        qg = sb("qg", [128, 3], F32); kvg = sb("kvg", [128, 2], F32)
        P.dma("sp", qg[:], qg_d); P.dma("sp", kvg[:], kvg_d)
        k = 0
        for kc in range(3):
            P.dma("sp", wst[k % 2][:], wuq_d[kc * 128:(kc + 1) * 128, :])
            P.copy("pool" if k % 2 else "dve", Wuq[:, kc, :], wst[k % 2][:]); k += 1
        for kc in range(2):
            P.dma("sp", wst[k % 2][:], wukv_d[kc * 128:(kc + 1) * 128, :])
            P.copy("pool" if k % 2 else "dve", Wukv[:, kc, :], wst[k % 2][:]); k += 1
        mqn = sb("mqn", [128, 3, T], BF16); mkvn = sb("mkvn", [128, 2, T], BF16); krT = sb("krT", [64, T], BF16)
        mq = sb("mq", [128, 3, 512], F32); sq = sb("sq", [128, 3, 512], F32); rn = sb("rn", [128, 512], F32)
        krt = sb("krt", [64, 512], F32); krp = sb("krp", [64, 512], F32)
        cs_ = [sb("cos%d" % i, [64, 512], F32) for i in range(2)]
        sn_ = [sb("sin%d" % i, [64, 512], F32) for i in range(2)]
        t1 = sb("t1", [64, 512], F32); t2 = sb("t2", [64, 512], F32)
        ps_s = [ps("pss%d" % i, [128, 512], F32) for i in range(2)]
        acc = [ps("acc%d" % i, [128, 512], F32) for i in range(4)]
        pq = ps("pq", [128, 512], F32)
        ptr = ps("ptr", [128, 1024], BF16)
        for tt in range(NT):
            ts_ = slice(tt * 512, (tt + 1) * 512)
            for (c0, nchk, dst, gain, dim) in ((cfg.c_mq, 3, mqn, qg, 384.0), (cfg.c_mkv, 2, mkvn, kvg, 256.0)):
                P.dma("sp", mq[:, 0:nchk, :], projT[c0 * 128:(c0 + nchk) * 128, ts_].rearrange("(c p) t -> p c t", p=128))
                P.tt("pool", sq[:, 0:nchk, :], mq[:, 0:nchk, :], mq[:, 0:nchk, :], ALU.mult)
                for kc in range(nchk):
                    P.mm(pq[:], ones_f[:], sq[:, kc, :], start=(kc == 0), stop=(kc == nchk - 1))
                P.act(rn[:], pq[:], AF.Sqrt, bias=EPS, scale=1.0 / dim)
                P.recip(rn[:], rn[:])
                for kc in range(nchk):
                    P.stt(dst[:, kc, ts_], mq[:, kc, :], gain[:, kc:kc + 1], rn[:], ALU.mult, ALU.mult)
            b = tt % 2
            P.dma("sp", krt[:], projT[cfg.c_mkr * 128:cfg.c_mkr * 128 + 64, ts_])
            P.dma("sp", krp[:], projT[cfg.c_mkr * 128 + 64:cfg.c_mkr * 128 + 128, ts_])
            P.dma("sp", cs_[b][:], cos_d[:, ts_]); P.dma("sp", sn_[b][:], sin_d[:, ts_])
            P.tt("dve", t1[:], krt[:], cs_[b][:], ALU.mult)
            P.tt("pool", t2[:], krp[:], sn_[b][:], ALU.mult)
            P.tt("dve", krT[:, ts_], t1[:], t2[:], ALU.add)
        knT = sb("knT", [128, HMG, T], BF16)
        Vsb = sb("Vsb", [128, NB, HMG, 129], BF16)
        qn = [sb("qn%d" % i, [128, 512], BF16) for i in range(2)]
        qr = [sb("qr%d" % i, [64, 512], BF16) for i in range(2)]
        mzt = [sb("mzt%d" % i, [128, 512], F32) for i in range(2)]
        PT = [sb("PT%d" % i, [128, 512], BF16) for i in range(2)]
        rs = sb("rs", [128, 4], F32)
        osb = [sb("osb%d" % i, [128, 128], BF16) for i in range(2)]
        obuf = [sb("obuf%d" % i, [128, 512], BF16) for i in range(2)]
        ei = 0
        for g in range(NG):
            P.memset("pool", Vsb[:], 1.0)
            for tt in range(NT):
                ts_ = slice(tt * 512, (tt + 1) * 512)
                for hl in range(HMG):
                    h = g * HMG + hl
                    pk = ps_s[hl % 2]
                    for kc in range(2):
                        P.mm(pk[:], Wukv[:, kc, h * 128:(h + 1) * 128], mkvn[:, kc, ts_], start=(kc == 0), stop=(kc == 1))
                    P.copy("act" if ei % 2 else "dve", knT[:, hl, ts_], pk[:]); ei += 1
                for s in range(4):
                    blk = tt * 4 + s
                    pv = acc[s % 2]
                    v0 = HM * 128 + g * HMG * 128
                    for kc in range(2):
                        P.mm(pv[:, 0:HMG * 128], mkvn[:, kc, blk * 128:(blk + 1) * 128], Wukv[:, kc, v0:v0 + HMG * 128],
                             start=(kc == 0), stop=(kc == 1))
                    P.copy("act" if ei % 2 else "dve", Vsb[:, blk, :, 0:128],
                           pv[:, 0:HMG * 128].rearrange("p (h d) -> p h d", h=HMG)); ei += 1
            it = 0
            for hl in range(HMG):
                h = g * HMG + hl
                for qt in range(NT):
                    b = it % 2; it += 1
                    ts_ = slice(qt * 512, (qt + 1) * 512)
                    for kc in range(3):
                        P.mm(pq[:], Wuq[:, kc, h * 128:(h + 1) * 128], mqn[:, kc, ts_], start=(kc == 0), stop=(kc == 2))
                    P.copy("dve", qn[b][:], pq[:])
                    P.dma("sp", cs_[b][:], cos_d[:, ts_]); P.dma("sp", sn_[b][:], sin_d[:, ts_])
                    r0 = HM * 128 + h * 64
                    for kc in range(3):
                        P.mm(pq[0:64, :], Wuq[:, kc, r0:r0 + 64], mqn[:, kc, ts_], start=(kc == 0), stop=(kc == 2))
                    P.tt("dve", t1[:], pq[0:64, :], cs_[b][:], ALU.mult)
                    r1 = HM * 192 + h * 64
                    for kc in range(3):
                        P.mm(pq[0:64, :], Wuq[:, kc, r1:r1 + 64], mqn[:, kc, ts_], start=(kc == 0), stop=(kc == 2))
                    P.tt("dve", t2[:], pq[0:64, :], sn_[b][:], ALU.mult)
                    P.tt("pool", qr[b][:], t1[:], t2[:], ALU.add)
                    P.dma("sp", mzt[b][:], projT[(cfg.c_mz + h) * 128:(cfg.c_mz + h + 1) * 128, ts_])
                    P.act(mzt[b][:], mzt[b][:], AF.Silu)
                    nkb = 4 * qt + 4
                    for kb in range(nkb):
                        r = kb - 4 * qt
                        qlo = max(r, 0) * 128; nq = 512 - qlo
                        pss = ps_s[kb % 2]; pt_ = PT[kb % 2]
                        ks = slice(kb * 128, (kb + 1) * 128)
                        P.mm(pss[:, 0:nq], knT[:, hl, ks], qn[b][:, qlo:512], start=True, stop=False)
                        P.mm(pss[:, 0:nq], krT[:, ks], qr[b][:, qlo:512], start=False, stop=True)
                        P.act(pt_[:, 0:nq], pss[:, 0:nq], AF.Exp, scale=SCALE)
                        if r >= 0:
                            P.tt("pool", pt_[:, 0:128], pt_[:, 0:128], triu_b[:], ALU.mult)
                        for s in range(qlo // 128, 4):
                            c0 = s * 128 - qlo
                            P.mm(acc[s][:, 0:129], pt_[:, c0:c0 + 128], Vsb[:, kb, hl, :], start=(kb == 0), stop=(kb == 4 * qt + s))
                    for s in range(4):
                        P.recip(rs[:, s:s + 1], acc[s][:, 128:129])
                        P.ts("dve", osb[s % 2][:], acc[s][:, 0:128], rs[:, s:s + 1], None, ALU.mult)
                        P.tr(ptr[:, 0:128], osb[s % 2][:], ident_b[:])
                        P.tt("dve", obuf[b][:, s * 128:(s + 1) * 128], ptr[:, 0:128], mzt[b][:, s * 128:(s + 1) * 128], ALU.mult)
                    row = (cfg.HG + h) * 128
                    P.dma("pool", mixT[row:row + 128, ts_], obuf[b][:])


def kernel(**inputs):
    inp = {k: np.asarray(v) for k, v in inputs.items()}
    T = 4096
    cfg = Cfg(range(4), range(8), range(4))
    hp = host_prep(inp, cfg, T)
    nc = build_nc(T, cfg, L=2, debug=False)
    in_maps = []
    for c in range(8):
        m = dict(hp)
        m["x"] = np.ascontiguousarray(inp["x"][c % 4], dtype=np.float32)
        in_maps.append(m)
    res = run_bass_kernel_spmd(nc, in_maps, core_ids=list(range(8)))
    out = np.stack([np.asarray(res.results[b]["out"]) for b in range(4)], axis=0)
    return out.astype(np.float32)
```
